# Optimizing a Trainium2 kernel written in Bass

```python
import jax
import jax.numpy as jnp
from jax import lax
import numpy as np

D_MODEL = 1024
BATCH = 4
SEQ = 8192
DEPTH = 4

HEAD_DIM = 64
N_HEADS = D_MODEL // HEAD_DIM
MIX_WIDTH = N_HEADS * HEAD_DIM
N_MIXERS = 3
Q_BLOCK = 128
ROPE_THETA = 10000.0
EPS = 1e-6
NEG_INF = -1e30
BIG = 1e30
SCALE = HEAD_DIM ** -0.5

NSA_KV_HEADS = 4
NSA_CMP_LEN = 32
NSA_CMP_STRIDE = 16
NSA_SLC_LEN = 64
NSA_TOPK = 16
NSA_WINDOW = 512
NSA_Q_BLOCK = 64
A_IN = MIX_WIDTH + 6 * NSA_KV_HEADS * HEAD_DIM + 3 * N_HEADS + MIX_WIDTH

SWA_KV_HEADS = 2
SWA_WINDOW = 128
B_IN = MIX_WIDTH + 2 * SWA_KV_HEADS * HEAD_DIM + MIX_WIDTH

C_IN = 3 * MIX_WIDTH + N_HEADS + MIX_WIDTH

kernel_name = 'hybrid_nsa_swasink_fox_trunk'


def layer_counts():
    return tuple(len(range(m, DEPTH, N_MIXERS)) for m in range(N_MIXERS))


def rms_norm(x, gain):
    xf = x.astype(jnp.float32)
    y = xf * lax.rsqrt(jnp.mean(xf * xf, axis=-1, keepdims=True) + EPS)
    return (y * gain.astype(jnp.float32)).astype(x.dtype)


def rope(x, positions):
    half = HEAD_DIM // 2
    inv_freq = ROPE_THETA ** (-jnp.arange(half, dtype=jnp.float32) * 2.0 / HEAD_DIM)
    ang = positions.astype(jnp.float32)[:, :, None, None] * inv_freq
    cos, sin = jnp.cos(ang), jnp.sin(ang)
    xf = x.astype(jnp.float32)
    x1, x2 = xf[..., :half], xf[..., half:]
    return jnp.concatenate([x1 * cos - x2 * sin, x2 * cos + x1 * sin], axis=-1).astype(x.dtype)


def split_heads(t, n):
    b, s, _ = t.shape
    return t.reshape(b, s, n, HEAD_DIM)


def q_groups(q, g):
    b, s, h, d = q.shape
    return q.reshape(b, s, g, h // g, d).transpose(0, 2, 3, 1, 4)


def kv_groups(k):
    return k.transpose(0, 2, 1, 3)


def merge_blocks(o):
    n, b, g, r, qb, d = o.shape
    return o.transpose(1, 0, 4, 2, 3, 5).reshape(b, n * qb, g * r * d)


def masked_softmax(s, mask):
    p = jax.nn.softmax(jnp.where(mask, s, NEG_INF), axis=-1)
    return jnp.where(mask, p, 0.0)


def sink_softmax(s, mask, sink):
    s = jnp.where(mask, s, NEG_INF)
    m = jnp.maximum(jnp.max(s, axis=-1, keepdims=True), sink)
    e = jnp.exp(s - m)
    return e / (jnp.sum(e, axis=-1, keepdims=True) + jnp.exp(sink - m))


def window_attend(qb, k_pad, v_pad, q0, t, window, sink=None):
    span = qb.shape[3] + window
    kb = lax.dynamic_slice_in_dim(k_pad, q0, span, axis=2)
    vb = lax.dynamic_slice_in_dim(v_pad, q0, span, axis=2)
    s_pos = q0 - window + jnp.arange(span)
    diff = t[:, None] - s_pos[None, :]
    mask = (diff >= 0) & (diff < window) & (s_pos[None, :] >= 0)
    s = jnp.einsum('bgrqd,bgkd->bgrqk', qb, kb).astype(jnp.float32) * SCALE
    p = masked_softmax(s, mask) if sink is None else sink_softmax(s, mask, sink)
    return jnp.einsum('bgrqk,bgkd->bgrqd', p.astype(vb.dtype), vb)


def nsa_mixer(h, positions, w_in, q_gain, k_gain, cmp_pos, cmp_w1, cmp_w2, w_out):
    b, s, _ = h.shape
    g = NSA_KV_HEADS
    kvd = g * HEAD_DIM
    sizes = [MIX_WIDTH] + [kvd] * 6 + [3 * N_HEADS]
    q, kc, vc, ks, vs, kw, vw, gl, z = jnp.split(h @ w_in, np.cumsum(sizes).tolist(), axis=-1)
    q = q_groups(rope(rms_norm(split_heads(q, N_HEADS), q_gain), positions), g)

    n_cmp = (s - NSA_CMP_LEN) // NSA_CMP_STRIDE + 1
    cmp_idx = np.arange(n_cmp)[:, None] * NSA_CMP_STRIDE + np.arange(NSA_CMP_LEN)[None, :]
    kc = kv_groups(rope(split_heads(kc, g), positions))
    vc = kv_groups(split_heads(vc, g))

    def compress(t, pos_emb, w1, w2):
        blocks = t[:, :, cmp_idx] + pos_emb
        flat = blocks.reshape(b, g, n_cmp, NSA_CMP_LEN * HEAD_DIM)
        return jax.nn.gelu(flat @ w1) @ w2

    k_cmp = rms_norm(compress(kc, cmp_pos[0], cmp_w1[0], cmp_w2[0]), k_gain[0])
    v_cmp = compress(vc, cmp_pos[1], cmp_w1[1], cmp_w2[1])
    cmp_end = jnp.asarray(cmp_idx[:, -1])

    n_blk = s // NSA_SLC_LEN
    top_n = min(NSA_TOPK, n_blk)
    slc_start = np.arange(n_blk) * NSA_SLC_LEN
    ov = np.minimum(cmp_idx[:, -1][:, None], slc_start[None, :] + NSA_SLC_LEN - 1) - np.maximum(cmp_idx[:, 0][:, None], slc_start[None, :]) + 1
    overlap = jnp.asarray(np.clip(ov, 0, None) / NSA_CMP_LEN, jnp.float32)
    k_slc = kv_groups(rope(rms_norm(split_heads(ks, g), k_gain[1]), positions)).reshape(b, g, n_blk, NSA_SLC_LEN, HEAD_DIM)
    v_slc = kv_groups(split_heads(vs, g)).reshape(b, g, n_blk, NSA_SLC_LEN, HEAD_DIM)
    blk_ids = jnp.arange(n_blk)
    gather = jax.vmap(jax.vmap(lambda blocks, ids: blocks[ids]))

    pad = ((0, 0), (0, 0), (NSA_WINDOW, 0), (0, 0))
    k_win = jnp.pad(kv_groups(rope(rms_norm(split_heads(kw, g), k_gain[2]), positions)), pad)
    v_win = jnp.pad(kv_groups(split_heads(vw, g)), pad)

    def block_fn(i):
        q0 = i * NSA_Q_BLOCK
        t = q0 + jnp.arange(NSA_Q_BLOCK)
        qb = lax.dynamic_slice_in_dim(q, q0, NSA_Q_BLOCK, axis=3)
        sc = jnp.einsum('bgrqd,bgcd->bgrqc', qb, k_cmp).astype(jnp.float32) * SCALE
        pc = masked_softmax(sc, cmp_end[None, :] <= t[:, None])
        o_cmp = jnp.einsum('bgrqc,bgcd->bgrqd', pc.astype(v_cmp.dtype), v_cmp)
        imp = jnp.einsum('bgrqc,cn->bgqn', pc, overlap)
        cur = t // NSA_SLC_LEN
        forced = (blk_ids[None, :] == 0) | (blk_ids[None, :] == cur[:, None]) | (blk_ids[None, :] == cur[:, None] - 1)
        imp = jnp.where(forced, BIG, jnp.where(blk_ids[None, :] > cur[:, None], NEG_INF, imp))
        _, sel = lax.top_k(imp, top_n)
        k_sel = gather(k_slc, sel).reshape(b, g, NSA_Q_BLOCK, top_n * NSA_SLC_LEN, HEAD_DIM)
        v_sel = gather(v_slc, sel).reshape(b, g, NSA_Q_BLOCK, top_n * NSA_SLC_LEN, HEAD_DIM)
        pos_sel = (sel[..., None] * NSA_SLC_LEN + jnp.arange(NSA_SLC_LEN)).reshape(b, g, NSA_Q_BLOCK, top_n * NSA_SLC_LEN)
        ss = jnp.einsum('bgrqd,bgqkd->bgrqk', qb, k_sel).astype(jnp.float32) * SCALE
        ps = masked_softmax(ss, (pos_sel <= t[:, None])[:, :, None])
        o_slc = jnp.einsum('bgrqk,bgqkd->bgrqd', ps.astype(v_sel.dtype), v_sel)
        o_win = window_attend(qb, k_win, v_win, q0, t, NSA_WINDOW)
        return o_cmp, o_slc, o_win

    o_cmp, o_slc, o_win = lax.map(block_fn, jnp.arange(s // NSA_Q_BLOCK))
    gates = jax.nn.sigmoid(gl.astype(jnp.float32)).reshape(b, s, 3, N_HEADS, 1).astype(h.dtype)
    o = (gates[:, :, 0] * merge_blocks(o_cmp).reshape(b, s, N_HEADS, HEAD_DIM)
         + gates[:, :, 1] * merge_blocks(o_slc).reshape(b, s, N_HEADS, HEAD_DIM)
         + gates[:, :, 2] * merge_blocks(o_win).reshape(b, s, N_HEADS, HEAD_DIM))
    return (o.reshape(b, s, MIX_WIDTH) * jax.nn.silu(z)) @ w_out


def swa_sink_mixer(h, positions, w_in, q_gain, k_gain, sinks, w_out):
    b, s, _ = h.shape
    g = SWA_KV_HEADS
    kvd = g * HEAD_DIM
    q, k, v, z = jnp.split(h @ w_in, np.cumsum([MIX_WIDTH, kvd, kvd]).tolist(), axis=-1)
    q = q_groups(rope(rms_norm(split_heads(q, N_HEADS), q_gain), positions), g)
    pad = ((0, 0), (0, 0), (SWA_WINDOW, 0), (0, 0))
    k = jnp.pad(kv_groups(rope(rms_norm(split_heads(k, g), k_gain), positions)), pad)
    v = jnp.pad(kv_groups(split_heads(v, g)), pad)
    sink = sinks.astype(jnp.float32).reshape(1, g, N_HEADS // g, 1, 1)

    def block_fn(i):
        q0 = i * Q_BLOCK
        t = q0 + jnp.arange(Q_BLOCK)
        qb = lax.dynamic_slice_in_dim(q, q0, Q_BLOCK, axis=3)
        return window_attend(qb, k, v, q0, t, SWA_WINDOW, sink)

    o = merge_blocks(lax.map(block_fn, jnp.arange(s // Q_BLOCK)))
    return (o * jax.nn.silu(z)) @ w_out


def fox_mixer(h, w_in, forget_bias, q_gain, k_gain, w_out):
    b, s, _ = h.shape
    q, k, v, fl, z = jnp.split(h @ w_in, np.cumsum([MIX_WIDTH] * 3 + [N_HEADS]).tolist(), axis=-1)
    q = q_groups(rms_norm(split_heads(q, N_HEADS), q_gain), N_HEADS)
    k = kv_groups(rms_norm(split_heads(k, N_HEADS), k_gain))
    v = kv_groups(split_heads(v, N_HEADS))
    log_f = jax.nn.log_sigmoid(fl.astype(jnp.float32) + forget_bias.astype(jnp.float32))
    cum = jnp.cumsum(log_f, axis=1).transpose(0, 2, 1)
    s_pos = jnp.arange(s)

    def block_fn(i):
        q0 = i * Q_BLOCK
        t = q0 + jnp.arange(Q_BLOCK)
        qb = lax.dynamic_slice_in_dim(q, q0, Q_BLOCK, axis=3)
        cq = lax.dynamic_slice_in_dim(cum, q0, Q_BLOCK, axis=2)
        decay = cq[:, :, None, :, None] - cum[:, :, None, None, :]
        sc = jnp.einsum('bgrqd,bgkd->bgrqk', qb, k).astype(jnp.float32) * SCALE + decay
        p = masked_softmax(sc, s_pos[None, :] <= t[:, None])
        return jnp.einsum('bgrqk,bgkd->bgrqd', p.astype(v.dtype), v)

    o = merge_blocks(lax.map(block_fn, jnp.arange(s // Q_BLOCK)))
    return (o * jax.nn.silu(z)) @ w_out


def setup_inputs(seed: int = 0) -> dict:
    key = jax.random.key(seed)
    keys = iter(jax.random.split(key, 32))
    n_a, n_b, n_c = layer_counts()

    def nrm(shape, scale):
        return scale * jax.random.normal(next(keys), shape, jnp.float32)

    def gain(shape):
        return 1.0 + nrm(shape, 0.02)

    cmp_flat = NSA_CMP_LEN * HEAD_DIM
    return {
        'x': nrm((BATCH, SEQ, D_MODEL), 1.0),
        'positions': jnp.broadcast_to(jnp.arange(SEQ, dtype=jnp.int32), (BATCH, SEQ)),
        'norm_gains': gain((DEPTH, D_MODEL)),
        'a_w_in': nrm((n_a, D_MODEL, A_IN), D_MODEL ** -0.5),
        'a_q_gain': gain((n_a, HEAD_DIM)),
        'a_k_gain': gain((n_a, 3, HEAD_DIM)),
        'a_cmp_pos': nrm((n_a, 2, NSA_CMP_LEN, HEAD_DIM), 0.1),
        'a_cmp_w1': nrm((n_a, 2, cmp_flat, HEAD_DIM), cmp_flat ** -0.5),
        'a_cmp_w2': nrm((n_a, 2, HEAD_DIM, HEAD_DIM), HEAD_DIM ** -0.5),
        'a_w_out': nrm((n_a, MIX_WIDTH, D_MODEL), MIX_WIDTH ** -0.5),
        'b_w_in': nrm((n_b, D_MODEL, B_IN), D_MODEL ** -0.5),
        'b_q_gain': gain((n_b, HEAD_DIM)),
        'b_k_gain': gain((n_b, HEAD_DIM)),
        'b_sinks': nrm((n_b, N_HEADS), 0.5),
        'b_w_out': nrm((n_b, MIX_WIDTH, D_MODEL), MIX_WIDTH ** -0.5),
        'c_w_in': nrm((n_c, D_MODEL, C_IN), D_MODEL ** -0.5),
        'c_forget_bias': 4.0 + nrm((n_c, N_HEADS), 0.5),
        'c_q_gain': gain((n_c, HEAD_DIM)),
        'c_k_gain': gain((n_c, HEAD_DIM)),
        'c_w_out': nrm((n_c, MIX_WIDTH, D_MODEL), MIX_WIDTH ** -0.5),
    }


def reference(x, positions, norm_gains, a_w_in, a_q_gain, a_k_gain, a_cmp_pos, a_cmp_w1, a_cmp_w2, a_w_out,
              b_w_in, b_q_gain, b_k_gain, b_sinks, b_w_out,
              c_w_in, c_forget_bias, c_q_gain, c_k_gain, c_w_out):
    for i in range(DEPTH):
        j = i // N_MIXERS
        h = rms_norm(x, norm_gains[i])
        mixer = i % N_MIXERS
        if mixer == 0:
            y = nsa_mixer(h, positions, a_w_in[j], a_q_gain[j], a_k_gain[j], a_cmp_pos[j],
                          a_cmp_w1[j], a_cmp_w2[j], a_w_out[j])
        elif mixer == 1:
            y = swa_sink_mixer(h, positions, b_w_in[j], b_q_gain[j], b_k_gain[j], b_sinks[j], b_w_out[j])
        else:
            y = fox_mixer(h, c_w_in[j], c_forget_bias[j], c_q_gain[j], c_k_gain[j], c_w_out[j])
        x = x + y
    return x
```

```python
import contextlib
import numpy as np
import ml_dtypes
import concourse.bass as bass
import concourse.mybir as mybir
from concourse.bass_utils import run_bass_kernel_spmd

F32 = mybir.dt.float32
BF16 = mybir.dt.bfloat16
I32 = mybir.dt.int32
AF = mybir.ActivationFunctionType
ALU = mybir.AluOpType
AX = mybir.AxisListType

S = 8192
D = 1024
NT = S // 128
HD = 64
HL = 8
EPS = 1e-6
NEG = -30000.0
NDS = 48
DEBUG_SCRATCH = False
EXPERIMENT_A = False


class Buf:
    __slots__ = ("w", "r", "name")

    def __init__(self, name=""):
        self.w = None
        self.r = {}
        self.name = name


class Prog:
    def __init__(self):
        self.nc = bass.Bass("TRN2", target_bir_lowering=False)
        nc = self.nc
        self.es = contextlib.ExitStack()
        self.eng = {"pe": nc.tensor, "act": nc.scalar, "dve": nc.vector, "pool": nc.gpsimd, "sp": nc.sync}
        self.csem = {}
        self.ccnt = {}
        for e in ("pe", "act", "dve", "pool"):
            self.csem[e] = self.es.enter_context(nc.semaphore("s_" + e))
            self.ccnt[e] = 0
        self.dsem = [self.es.enter_context(nc.semaphore("d%d" % i)) for i in range(NDS)]
        self.dval = [0] * NDS
        self.dnext = 0
        self.waited = {e: {} for e in self.eng}
        self.nbuf = 0

    def sb(self, name, shape, dt):
        return self.es.enter_context(self.nc.sbuf_tensor(name, list(shape), dt))

    def ps(self, name, shape, dt):
        return self.es.enter_context(self.nc.psum_tensor(name, list(shape), dt))

    def din(self, name, shape, dt):
        return self.nc.dram_tensor(name, list(shape), dt, kind="ExternalInput").ap()

    def dout(self, name, shape, dt):
        return self.nc.dram_tensor(name, list(shape), dt, kind="ExternalOutput").ap()

    def dscr(self, name, shape, dt):
        if DEBUG_SCRATCH:
            return self.nc.dram_tensor(name, list(shape), dt, kind="ExternalOutput").ap()
        return self.nc.dram_tensor(name, list(shape), dt).ap()

    def buf(self, name=""):
        return Buf(name)

    def _wait(self, e, tok):
        kind, key, val = tok
        k = (kind, key)
        if self.waited[e].get(k, 0) >= val:
            return
        sem = self.csem[key] if kind == "c" else self.dsem[key]
        self.eng[e].wait_ge(sem, val)
        self.waited[e][k] = val

    def _deps(self, e, reads, writes):
        for b in reads:
            t = b.w
            if t is not None:
                if t[0] == "c" and t[1] == e and e == "pe":
                    continue
                self._wait(e, t)
        for b in writes:
            t = b.w
            if t is not None and not (t[0] == "c" and t[1] == e):
                self._wait(e, t)
            for (kind, key), val in b.r.items():
                if kind == "c" and key == e:
                    continue
                self._wait(e, (kind, key, val))

    def _mark(self, tok, reads, writes):
        for b in reads:
            k = (tok[0], tok[1])
            if b.r.get(k, 0) < tok[2]:
                b.r[k] = tok[2]
        for b in writes:
            b.w = tok
            b.r = {}

    def op(self, e, fn, reads=(), writes=()):
        self._deps(e, reads, writes)
        ins = fn(self.eng[e])
        ins.then_inc(self.csem[e], 1)
        self.ccnt[e] += 1
        tok = ("c", e, self.ccnt[e])
        self._mark(tok, reads, writes)
        return tok

    def dma(self, e, out, in_, reads=(), writes=(), **kw):
        self._deps(e, reads, writes)
        i = self.dnext
        self.dnext = (self.dnext + 1) % NDS
        self._wait(e, ("d", i, self.dval[i]))
        self.dval[i] += 16
        self.eng[e].dma_start(out=out, in_=in_, **kw).then_inc(self.dsem[i], 16)
        tok = ("d", i, self.dval[i])
        self._mark(tok, reads, writes)
        return tok

    def barrier(self):
        for e in self.eng:
            for e2 in self.ccnt:
                if e2 != e and self.ccnt[e2] > 0:
                    self._wait(e, ("c", e2, self.ccnt[e2]))
            for i in range(NDS):
                if self.dval[i] > 0:
                    self._wait(e, ("d", i, self.dval[i]))

    def finish(self):
        for i in range(NDS):
            if self.dval[i] > 0:
                self._wait("sp", ("d", i, self.dval[i]))
        self.es.close()
        return self.nc


KINDS = {
    "nsa": dict(ncols=1816, chunks=[(0, 512), (512, 384), (896, 408), (1304, 512)],
                titems=[(0, 8, 0, True), (512, 2, None, True), (640, 2, 1, True), (768, 2, 2, True), (896, 2, None, False)],
                ngain=3, vcol=1024, nvh=4, zcol=1304, gcol=1280, ng=24),
    "swa": dict(ncols=1152, chunks=[(0, 512), (512, 128), (640, 512)],
                titems=[(0, 8, 0, True), (512, 1, 1, True)],
                ngain=2, vcol=576, nvh=1, zcol=640, gcol=None, ng=0),
    "fox": dict(ncols=2056, chunks=[(0, 512), (512, 512), (1024, 512), (1536, 8), (1544, 512)],
                titems=[(0, 8, 0, False), (512, 8, 1, False)],
                ngain=2, vcol=1024, nvh=8, zcol=1544, gcol=1536, ng=8),
}


def tcols(kind):
    return sum(nh * 64 for (_, nh, _, _) in KINDS[kind]["titems"])


def phase_p(P, kind, with_prev, io, scr, consts):
    nc = P.nc
    cfg = KINDS[kind] if kind else None
    es = contextlib.ExitStack()

    def sb(name, shape, dt):
        return es.enter_context(nc.sbuf_tensor(name, list(shape), dt))

    def psb(name):
        return es.enter_context(nc.psum_tensor(name, [128, 512], F32))

    ident = consts["ident"]
    bT = [psb("pT%d" % i) for i in range(2)]
    bTb = [P.buf() for _ in range(2)]
    bM = [psb("pM%d" % i) for i in range(4)]
    bMb = [P.buf() for _ in range(4)]
    mrot = [0]

    def next_bank():
        i = mrot[0]
        mrot[0] = (i + 1) % 4
        return bM[i], bMb[i]

    xt = [sb("xt%d" % i, [128, D], F32) for i in range(2)]
    xtb = [P.buf() for _ in range(2)]
    junk = sb("junk", [128, D], F32)
    junkb = P.buf()
    stat = sb("stat", [128, 64], F32)
    statb = P.buf()
    wst = sb("wst", [128, 8, 256], F32)
    wstb = P.buf()
    if with_prev:
        ogt = [sb("ogt%d" % i, [128, D], BF16) for i in range(2)]
        ogtb = [P.buf() for _ in range(2)]
        ogT = sb("ogT", [128, D], BF16)
        ogTb = P.buf()
        wout = sb("wout", [128, 8, D], BF16)
        woutb = P.buf()
        wv = io["w_out"].rearrange("(k p) n -> p k n", p=128)
        for j in range(4):
            P.dma("sp", wst[:], wv[:, :, j * 256:(j + 1) * 256], writes=[wstb])
            P.op("pool", lambda e, j=j: e.tensor_copy(out=wout[:, :, j * 256:(j + 1) * 256], in_=wst[:]),
                 reads=[wstb], writes=[woutb])
    if kind:
        ncols = cfg["ncols"]
        TC = tcols(kind)
        nblk = (TC + 127) // 128
        hb = sb("hb", [128, D], BF16)
        hbb = P.buf()
        hT = sb("hT", [128, D], BF16)
        hTb = P.buf()
        wsb = sb("wsb", [128, 8, ncols], BF16)
        wsbb = P.buf()
        wv = io["w_in"].rearrange("(k p) n -> p k n", p=128)
        c0 = 0
        while c0 < ncols:
            w = min(256, ncols - c0)
            P.dma("sp", wst[:, :, 0:w], wv[:, :, c0:c0 + w], writes=[wstb])
            P.op("pool", lambda e, c0=c0, w=w: e.tensor_copy(out=wsb[:, :, c0:c0 + w], in_=wst[:, :, 0:w]),
                 reads=[wstb], writes=[wsbb])
            c0 += w
        proj = [sb("proj%d" % i, [128, ncols], F32) for i in range(2)]
        projb = [P.buf() for _ in range(2)]
        gB = sb("gB", [128, D], F32)
        gBb = P.buf()
        P.dma("sp", gB[:], io["ng"].partition_broadcast(128), writes=[gBb])
        ngain = cfg["ngain"]
        hg = sb("hg", [128, ngain, 64], F32)
        hgb = P.buf()
        P.dma("sp", hg[:], io["gains"].partition_broadcast(128), writes=[hgb])
        P.op("dve", lambda e: e.tensor_scalar(out=hg[:, 0, :], in0=hg[:, 0, :], scalar1=0.125, scalar2=None, op0=ALU.mult),
             reads=[hgb], writes=[hgb])
        need_rope = any(r for (_, _, _, r) in cfg["titems"])
        if need_rope:
            cs = sb("cs", [128, NT, 32], F32)
            sn = sb("sn", [128, NT, 32], F32)
            csb = P.buf()
            posi = sb("posi", [128, NT], I32)
            posf = sb("posf", [128, NT], F32)
            invf = sb("invf", [128, 32], F32)
            ang = sb("ang", [128, NT, 32], F32)
            tb_ = P.buf()
            P.dma("sp", posi[:], io["pos"].rearrange("(t p) -> p t", p=128), writes=[tb_], allow_slow_non_contiguous=True)
            P.dma("sp", invf[:], consts["invf_d"].partition_broadcast(128), writes=[tb_])
            P.op("dve", lambda e: e.tensor_copy(out=posf[:], in_=posi[:]), reads=[tb_], writes=[tb_])
            P.op("dve", lambda e: e.tensor_tensor(out=ang[:], in0=posf[:].unsqueeze(2).to_broadcast([128, NT, 32]),
                                                  in1=invf[:].unsqueeze(1).to_broadcast([128, NT, 32]), op=ALU.mult),
                 reads=[tb_], writes=[tb_])
            TWO_PI = 2.0 * np.pi
            SC = 1.0 - 2e-6
            C1 = 6.28125
            C2 = TWO_PI - C1
            ki = sb("ki", [128, NT, 32], I32)
            kf = sb("kf", [128, NT, 32], F32)
            aa = sb("aa", [128, NT, 32], F32)
            mm = sb("mm", [128, NT, 32], F32)
            for (dst, off) in ((sn, 0.0), (cs, 0.5 * np.pi)):
                P.op("dve", lambda e, off=off: e.tensor_scalar(out=aa[:], in0=ang[:], scalar1=float(off), scalar2=None, op0=ALU.add), reads=[tb_], writes=[tb_])
                P.op("dve", lambda e: e.tensor_scalar(out=ki[:], in0=aa[:], scalar1=float(1.0 / TWO_PI), scalar2=None, op0=ALU.mult), reads=[tb_], writes=[tb_])
                P.op("dve", lambda e: e.tensor_copy(out=kf[:], in_=ki[:]), reads=[tb_], writes=[tb_])
                P.op("dve", lambda e: e.scalar_tensor_tensor(out=aa[:], in0=kf[:], scalar=float(-C1), in1=aa[:], op0=ALU.mult, op1=ALU.add), reads=[tb_], writes=[tb_])
                P.op("dve", lambda e: e.scalar_tensor_tensor(out=aa[:], in0=kf[:], scalar=float(-C2), in1=aa[:], op0=ALU.mult, op1=ALU.add), reads=[tb_], writes=[tb_])
                P.op("dve", lambda e: e.tensor_scalar(out=mm[:], in0=aa[:], scalar1=float(np.pi), scalar2=float(-TWO_PI), op0=ALU.is_gt, op1=ALU.mult), reads=[tb_], writes=[tb_])
                P.op("dve", lambda e: e.tensor_tensor(out=aa[:], in0=aa[:], in1=mm[:], op=ALU.add), reads=[tb_], writes=[tb_])
                P.op("dve", lambda e: e.tensor_scalar(out=mm[:], in0=aa[:], scalar1=float(-np.pi), scalar2=float(TWO_PI), op0=ALU.is_lt, op1=ALU.mult), reads=[tb_], writes=[tb_])
                P.op("dve", lambda e: e.tensor_tensor(out=aa[:], in0=aa[:], in1=mm[:], op=ALU.add), reads=[tb_], writes=[tb_])
                P.op("act", lambda e, dst=dst: e.activation(out=dst[:], in_=aa[:], func=AF.Sin, scale=SC), reads=[tb_], writes=[csb])
        sq = sb("sq", [128, 512], F32)
        sqb = P.buf()
        tmpn = sb("tmpn", [128, 512], F32)
        tmpnb = P.buf()
        ra = sb("ra", [128, 256], F32)
        rb_ = sb("rb_", [128, 256], F32)
        rab = P.buf()
        tsrc = sb("tsrc", [128, nblk * 128], BF16)
        tsrcb = P.buf()
        tstage = [sb("tstage%d" % i, [128, nblk, 512], BF16) for i in range(2)]
        tstageb = [P.buf() for _ in range(2)]
        nvh = cfg["nvh"]
        vstage = [sb("vstage%d" % i, [128, 4, nvh, 65], BF16) for i in range(2)]
        vstageb = [P.buf() for _ in range(2)]
        for i in range(2):
            P.op("pool", lambda e, i=i: e.memset(vstage[i][:], 1.0), writes=[vstageb[i]])
        szst = [sb("szst%d" % i, [128, 4, 512], BF16) for i in range(2)]
        szstb = [P.buf() for _ in range(2)]
        ez = sb("ez", [128, 512], F32)
        ezb = P.buf()
        ng = cfg["ng"]
        if kind == "nsa":
            gst = [sb("gst%d" % i, [128, 4, ng], F32) for i in range(2)]
            gstb = [P.buf() for _ in range(2)]
            eg = sb("eg", [128, ng], F32)
            egb = P.buf()
        if kind == "fox":
            fbB = sb("fbB", [128, 8], F32)
            fbBb = P.buf()
            P.dma("sp", fbB[:], io["fbias"].partition_broadcast(128), writes=[fbBb])
            lf = sb("lf", [128, 8], F32)
            lfb = P.buf()
            cum = scr["cum"]
            cumE = scr["cumE"]
            cumb = scr["cumb"]
            triu = consts["triu_f"]
            onesf = consts["ones_f"]

    def stage1(t):
        i = t % 2
        rows = slice(t * 128, (t + 1) * 128)
        P.dma("sp", xt[i][:], io["x"][rows, :], writes=[xtb[i]])
        if with_prev:
            P.dma("sp", ogt[i][:], io["ogp"][rows, :], writes=[ogtb[i]])
            pT, pTb = bT[0], bTb[0]
            pTv = pT[:].bitcast(BF16)
            for k in range(8):
                P.op("pe", lambda e, k=k: e.transpose(out=pTv[:, k * 128:(k + 1) * 128], in_=ogt[i][:, k * 128:(k + 1) * 128], identity=ident[:]),
                     reads=[ogtb[i]], writes=[pTb])
            P.op("act", lambda e: e.copy(out=ogT[:], in_=pTv[:, 0:D]), reads=[pTb], writes=[ogTb])
            for hf in range(2):
                pm, pmb = next_bank()
                for k in range(8):
                    P.op("pe", lambda e, k=k, hf=hf, pm=pm: e.matmul(pm[:], lhsT=ogT[:, k * 128:(k + 1) * 128], rhs=wout[:, k, hf * 512:(hf + 1) * 512],
                                                                      start=(k == 0), stop=(k == 7)),
                         reads=[ogTb, woutb], writes=[pmb])
                P.op("dve", lambda e, hf=hf, pm=pm: e.tensor_tensor(out=xt[i][:, hf * 512:(hf + 1) * 512], in0=xt[i][:, hf * 512:(hf + 1) * 512], in1=pm[:], op=ALU.add),
                     reads=[pmb, xtb[i]], writes=[xtb[i]])
            P.dma("pool", io["xn"][rows, :], xt[i][:], reads=[xtb[i]])
        if not kind:
            return
        P.op("act", lambda e: e.activation(out=junk[:], in_=xt[i][:], func=AF.Square), reads=[xtb[i]], writes=[junkb])
        P.op("dve", lambda e: e.tensor_reduce(out=stat[:, 0:1], in_=junk[:], axis=AX.X, op=ALU.add), reads=[junkb], writes=[statb])
        P.op("dve", lambda e: e.tensor_scalar(out=stat[:, 1:2], in0=stat[:, 0:1], scalar1=1.0 / D, scalar2=EPS, op0=ALU.mult, op1=ALU.add),
             reads=[statb], writes=[statb])
        P.op("act", lambda e: e.activation(out=stat[:, 3:4], in_=stat[:, 1:2], func=AF.Ln), reads=[statb], writes=[statb])
        P.op("act", lambda e: e.activation(out=stat[:, 2:3], in_=stat[:, 3:4], func=AF.Exp, scale=-0.5), reads=[statb], writes=[statb])
        P.op("dve", lambda e: e.scalar_tensor_tensor(out=hb[:], in0=xt[i][:], scalar=stat[:, 2:3], in1=gB[:], op0=ALU.mult, op1=ALU.mult),
             reads=[statb, xtb[i], gBb], writes=[hbb])
        pT, pTb = bT[1], bTb[1]
        pTv = pT[:].bitcast(BF16)
        for k in range(8):
            P.op("pe", lambda e, k=k: e.transpose(out=pTv[:, k * 128:(k + 1) * 128], in_=hb[:, k * 128:(k + 1) * 128], identity=ident[:]),
                 reads=[hbb], writes=[pTb])
        P.op("act", lambda e: e.copy(out=hT[:], in_=pTv[:, 0:D]), reads=[pTb], writes=[hTb])
        for (c0, w) in cfg["chunks"]:
            pm, pmb = next_bank()
            for k in range(8):
                P.op("pe", lambda e, k=k, pm=pm, c0=c0, w=w: e.matmul(pm[:, 0:w], lhsT=hT[:, k * 128:(k + 1) * 128], rhs=wsb[:, k, c0:c0 + w],
                                                                        start=(k == 0), stop=(k == 7)),
                     reads=[hTb, wsbb], writes=[pmb])
            P.op("act", lambda e, pm=pm, c0=c0, w=w: e.copy(out=proj[i][:, c0:c0 + w], in_=pm[:, 0:w]), reads=[pmb], writes=[projb[i]])

    def stage2(t):
        i = t % 2
        g = t // 4
        tq = t % 4
        gi = g % 2
        pj = proj[i]
        pjb = projb[i]
        T0 = 0
        for (c0, nh, gidx, rope) in cfg["titems"]:
            w = nh * 64
            src = pj[:, c0:c0 + w]
            srcb = pjb
            if gidx is not None:
                P.op("act", lambda e, src=src, w=w: e.activation(out=sq[:, 0:w], in_=src, func=AF.Square), reads=[pjb], writes=[sqb])
                P.op("dve", lambda e, w=w, nh=nh: e.tensor_reduce(out=stat[:, 8:8 + nh], in_=sq[:, 0:w].rearrange("p (h d) -> p h d", d=64), axis=AX.X, op=ALU.add),
                     reads=[sqb], writes=[statb])
                P.op("dve", lambda e, nh=nh: e.tensor_scalar(out=stat[:, 16:16 + nh], in0=stat[:, 8:8 + nh], scalar1=1.0 / 64, scalar2=EPS, op0=ALU.mult, op1=ALU.add),
                     reads=[statb], writes=[statb])
                P.op("act", lambda e, nh=nh: e.activation(out=stat[:, 32:32 + nh], in_=stat[:, 16:16 + nh], func=AF.Ln), reads=[statb], writes=[statb])
                P.op("act", lambda e, nh=nh: e.activation(out=stat[:, 24:24 + nh], in_=stat[:, 32:32 + nh], func=AF.Exp, scale=-0.5), reads=[statb], writes=[statb])
                P.op("dve", lambda e, src=src, w=w, nh=nh: e.tensor_tensor(out=tmpn[:, 0:w].rearrange("p (h d) -> p h d", d=64),
                                                                             in0=src.rearrange("p (h d) -> p h d", d=64),
                                                                             in1=stat[:, 24:24 + nh].unsqueeze(2).to_broadcast([128, nh, 64]), op=ALU.mult),
                     reads=[statb, pjb], writes=[tmpnb])
                dst = tmpn[:, 0:w] if rope else tsrc[:, T0:T0 + w]
                dstb = tmpnb if rope else tsrcb
                P.op("dve", lambda e, w=w, nh=nh, gidx=gidx, dst=dst: e.tensor_tensor(out=dst.rearrange("p (h d) -> p h d", d=64),
                                                                                      in0=tmpn[:, 0:w].rearrange("p (h d) -> p h d", d=64),
                                                                                      in1=hg[:, gidx, :].unsqueeze(1).to_broadcast([128, nh, 64]), op=ALU.mult),
                     reads=[tmpnb, hgb], writes=[dstb])
                src = tmpn[:, 0:w]
                srcb = tmpnb
            if rope:
                s3 = src.rearrange("p (h d) -> p h d", d=64)
                d3 = tsrc[:, T0:T0 + w].rearrange("p (h d) -> p h d", d=64)
                cB = cs[:, t, :].unsqueeze(1).to_broadcast([128, nh, 32])
                sB = sn[:, t, :].unsqueeze(1).to_broadcast([128, nh, 32])
                a3 = ra[:, 0:nh * 32].rearrange("p (h d) -> p h d", d=32)
                b3 = rb_[:, 0:nh * 32].rearrange("p (h d) -> p h d", d=32)
                x1 = s3[:, :, 0:32]
                x2 = s3[:, :, 32:64]
                P.op("dve", lambda e: e.tensor_tensor(out=a3, in0=x1, in1=cB, op=ALU.mult), reads=[srcb, csb], writes=[rab])
                P.op("dve", lambda e: e.tensor_tensor(out=b3, in0=x2, in1=sB, op=ALU.mult), reads=[srcb, csb], writes=[rab])
                P.op("dve", lambda e: e.tensor_tensor(out=d3[:, :, 0:32], in0=a3, in1=b3, op=ALU.subtract), reads=[rab], writes=[tsrcb])
                P.op("dve", lambda e: e.tensor_tensor(out=a3, in0=x2, in1=cB, op=ALU.mult), reads=[srcb, csb], writes=[rab])
                P.op("dve", lambda e: e.tensor_tensor(out=b3, in0=x1, in1=sB, op=ALU.mult), reads=[srcb, csb], writes=[rab])
                P.op("dve", lambda e: e.tensor_tensor(out=d3[:, :, 32:64], in0=a3, in1=b3, op=ALU.add), reads=[rab], writes=[tsrcb])
            elif gidx is None:
                P.op("act", lambda e, src=src, w=w, T0=T0: e.copy(out=tsrc[:, T0:T0 + w], in_=src), reads=[srcb], writes=[tsrcb])
            T0 += w
        pT, pTb = next_bank()
        pTv = pT[:].bitcast(BF16)
        for j in range(nblk):
            w = min(128, TC - j * 128)
            P.op("pe", lambda e, j=j, w=w: e.transpose(out=pTv[0:w, j * 128:(j + 1) * 128], in_=tsrc[:, j * 128:j * 128 + w], identity=ident[:]),
                 reads=[tsrcb], writes=[pTb])
        nfull = TC // 128
        P.op("act", lambda e: e.copy(out=tstage[gi][:, 0:nfull, tq * 128:(tq + 1) * 128],
                                     in_=pTv[:, 0:nfull * 128].rearrange("p (j q) -> p j q", q=128)),
             reads=[pTb], writes=[tstageb[gi]])
        if TC % 128:
            P.op("act", lambda e: e.copy(out=tstage[gi][0:64, nfull, tq * 128:(tq + 1) * 128], in_=pTv[0:64, nfull * 128:(nfull + 1) * 128]),
                 reads=[pTb], writes=[tstageb[gi]])
        vc = cfg["vcol"]
        P.op("act", lambda e: e.copy(out=vstage[gi][:, tq, :, 0:64], in_=pj[:, vc:vc + nvh * 64].rearrange("p (h d) -> p h d", d=64)),
             reads=[pjb], writes=[vstageb[gi]])
        zc = cfg["zcol"]
        P.op("act", lambda e: e.activation(out=ez[:], in_=pj[:, zc:zc + 512], func=AF.Exp, scale=-1.0), reads=[pjb], writes=[ezb])
        P.op("dve", lambda e: e.tensor_scalar(out=ez[:], in0=ez[:], scalar1=1.0, scalar2=None, op0=ALU.add), reads=[ezb], writes=[ezb])
        P.op("dve", lambda e: e.reciprocal(out=ez[:], in_=ez[:]), reads=[ezb], writes=[ezb])
        P.op("dve", lambda e: e.tensor_tensor(out=szst[gi][:, tq, :], in0=pj[:, zc:zc + 512], in1=ez[:], op=ALU.mult), reads=[ezb, pjb], writes=[szstb[gi]])
        if kind == "nsa":
            gc = cfg["gcol"]
            P.op("act", lambda e: e.activation(out=eg[:], in_=pj[:, gc:gc + ng], func=AF.Exp, scale=-1.0), reads=[pjb], writes=[egb])
            P.op("dve", lambda e: e.tensor_scalar(out=eg[:], in0=eg[:], scalar1=1.0, scalar2=None, op0=ALU.add), reads=[egb], writes=[egb])
            P.op("dve", lambda e: e.reciprocal(out=gst[gi][:, tq, :], in_=eg[:]), reads=[egb], writes=[gstb[gi]])
        if kind == "fox":
            gc = cfg["gcol"]
            P.op("dve", lambda e: e.tensor_tensor(out=lf[:], in0=pj[:, gc:gc + 8], in1=fbB[:], op=ALU.add), reads=[pjb, fbBb], writes=[lfb])
            P.op("act", lambda e: e.activation(out=lf[:], in_=lf[:], func=AF.Exp, scale=-1.0), reads=[lfb], writes=[lfb])
            P.op("act", lambda e: e.activation(out=lf[:], in_=lf[:], func=AF.Ln, bias=consts["one_col"][:, 0:1], scale=1.0), reads=[lfb], writes=[lfb])
            P.op("dve", lambda e: e.tensor_scalar(out=lf[:], in0=lf[:], scalar1=-1.0, scalar2=None, op0=ALU.mult), reads=[lfb], writes=[lfb])
            pm, pmb = next_bank()
            P.op("pe", lambda e, pm=pm: e.matmul(pm[:, 0:8], lhsT=triu[:], rhs=lf[:], start=True, stop=True), reads=[lfb], writes=[pmb])
            P.op("pe", lambda e, pm=pm: e.matmul(pm[:, 8:16], lhsT=onesf[:], rhs=lf[:], start=True, stop=True), reads=[lfb], writes=[pmb])
            if t == 0:
                P.op("dve", lambda e, pm=pm: e.tensor_copy(out=cum[:, t, :], in_=pm[:, 0:8]), reads=[pmb], writes=[cumb])
                P.op("dve", lambda e, pm=pm: e.tensor_copy(out=cumE[:, t, :], in_=pm[:, 8:16]), reads=[pmb], writes=[cumb])
            else:
                P.op("dve", lambda e, pm=pm: e.tensor_tensor(out=cum[:, t, :], in0=pm[:, 0:8], in1=cumE[:, t - 1, :], op=ALU.add), reads=[pmb, cumb], writes=[cumb])
                P.op("dve", lambda e, pm=pm: e.tensor_tensor(out=cumE[:, t, :], in0=pm[:, 8:16], in1=cumE[:, t - 1, :], op=ALU.add), reads=[pmb, cumb], writes=[cumb])
        if tq == 3:
            T0 = 0
            j = 0
            for (c0, nh, gidx, rope), dst in zip(cfg["titems"], scr["tdst"]):
                w = nh * 64
                nb = (w + 127) // 128
                for b in range(nb):
                    rw = min(128, w - b * 128)
                    P.dma("pool", dst[b * 128:b * 128 + rw, g * 512:(g + 1) * 512], tstage[gi][0:rw, j, :], reads=[tstageb[gi]])
                    j += 1
            P.dma("pool", scr["v"].rearrange("(t p) c -> p t c", p=128)[:, g * 4:(g + 1) * 4, :],
                  vstage[gi][:].rearrange("p t h c -> p t (h c)"), reads=[vstageb[gi]])
            P.dma("pool", scr["sz"].rearrange("(t p) c -> p t c", p=128)[:, g * 4:(g + 1) * 4, :], szst[gi][:], reads=[szstb[gi]])
            if kind == "nsa":
                P.dma("pool", scr["gates"].rearrange("(t p) c -> p t c", p=128)[:, g * 4:(g + 1) * 4, :], gst[gi][:], reads=[gstb[gi]])

    if kind:
        stage1(0)
        for t in range(NT):
            if t + 1 < NT:
                stage1(t + 1)
            stage2(t)
    else:
        for t in range(NT):
            stage1(t)
    P.barrier()
    es.close()


class AttnRes:
    def __init__(self, P, es, nv_max=65):
        nc = P.nc
        self.NS = 3
        self.NP = 3
        self.psS = [es.enter_context(nc.psum_tensor("psS%d" % i, [128, 512], F32)) for i in range(self.NS)]
        self.psSb = [P.buf() for _ in range(self.NS)]
        self.pT = [es.enter_context(nc.sbuf_tensor("pTs%d" % i, [128, 512], BF16)) for i in range(self.NP)]
        self.pTb = [P.buf() for _ in range(self.NP)]
        self.po = [es.enter_context(nc.psum_tensor("po%d" % i, [128, 512], F32)) for i in range(2)]
        self.pob = [P.buf() for _ in range(2)]


def run_jobs(P, jobs, res):
    flat = []
    for ji, job in enumerate(jobs):
        ns = len(job["steps"])
        for si, st in enumerate(job["steps"]):
            flat.append((ji, si, st, ns))
    state = {"pre": -1}

    def ensure_pre(j):
        j = min(j, len(jobs) - 1)
        while state["pre"] < j:
            state["pre"] += 1
            jobs[state["pre"]]["pre"]()

    def emit_qk(n):
        ji, si, st, ns = flat[n]
        ps = res.psS[n % res.NS]
        psb = res.psSb[n % res.NS]
        c0, c1 = st["cols"]
        masks = st.get("masks", [])
        P.op("pe", lambda e: e.matmul(ps[:, c0:c1], lhsT=st["kT"], rhs=st["q"][:, c0:c1], start=True, stop=(len(masks) == 0)),
             reads=st["kqb"], writes=[psb])
        for mi, (ml, mr, m0, m1, mb) in enumerate(masks):
            P.op("pe", lambda e, ml=ml, mr=mr, m0=m0, m1=m1, mi=mi: e.matmul(ps[:, m0:m1], lhsT=ml, rhs=mr, start=False, stop=(mi == len(masks) - 1)),
                 reads=mb, writes=[psb])

    def emit_exp_pv(n):
        ji, si, st, ns = flat[n]
        ps = res.psS[n % res.NS]
        psb = res.psSb[n % res.NS]
        pT = res.pT[n % res.NP]
        pTb = res.pTb[n % res.NP]
        c0, c1 = st["cols"]
        bias = st.get("bias")
        if bias is None:
            P.op("act", lambda e: e.activation(out=pT[:, c0:c1], in_=ps[:, c0:c1], func=AF.Exp), reads=[psb], writes=[pTb])
        else:
            P.op("act", lambda e: e.activation(out=pT[:, c0:c1], in_=ps[:, c0:c1], func=AF.Exp, bias=bias, scale=1.0),
                 reads=[psb] + st["biasb"], writes=[pTb])
        job = jobs[ji]
        po = res.po[job["po"]]
        pob = res.pob[job["po"]]
        nv = st["nv"]
        for i in st["subs"]:
            first = (si == 0 and i == st["subs"][0])
            last = job["last"][i] == si
            i0 = job.get("sub0", 0)
            P.op("pe", lambda e, i=i, i0=i0, first=first, last=last: e.matmul(po[:, (i - i0) * nv:(i - i0 + 1) * nv], lhsT=pT[:, i * 128:(i + 1) * 128], rhs=st["v"], start=first, stop=last,
                                                                              skip_group_check=True),
                 reads=[pTb] + st["vb"], writes=[pob])

    for job in jobs:
        first, last = {}, {}
        for si, st in enumerate(job["steps"]):
            for i in st["subs"]:
                first.setdefault(i, si)
                last[i] = si
        job["first"], job["last"] = first, last
    ensure_pre(1)
    emit_qk(0)
    for n in range(len(flat)):
        ji, si, st, ns = flat[n]
        if n + 1 < len(flat):
            ensure_pre(flat[n + 1][0] + 1)
            emit_qk(n + 1)
        emit_exp_pv(n)
        if si == ns - 1:
            jobs[ji]["post"]()


def window_steps(c, W, kT, kqb, q, v_of, vb, consts, nv=65):
    wd = W // 128
    steps = []
    for j in range(max(0, 4 * c - wd), 4 * c + 4):
        subs = [i for i in range(4) if 0 <= 4 * c + i - j <= wd]
        c0, c1 = subs[0] * 128, (subs[-1] + 1) * 128
        masks = []
        for i in subs:
            d = 4 * c + i - j
            if d == 0:
                masks.append((consts["ident"][:], consts["tri"][:], i * 128, (i + 1) * 128, []))
            elif d == wd:
                masks.append((consts["ident"][:], consts["tri2"][:], i * 128, (i + 1) * 128, []))
        steps.append(dict(kT=kT[:, j * 128:(j + 1) * 128], q=q, kqb=kqb, cols=(c0, c1), masks=masks, subs=subs, v=v_of(j), vb=vb, nv=nv))
    return steps


def load_consts(P, cd):
    consts = {}
    b = P.buf()
    for name, shape, dt in (("ident", [128, 128], BF16), ("tri", [128, 128], BF16), ("tri2", [128, 128], BF16),
                            ("triu_f", [128, 128], F32), ("ones_f", [128, 128], F32)):
        t = P.sb("k_" + name, shape, dt)
        P.dma("sp", t[:], cd[name], writes=[b])
        consts[name] = t
    oc = P.sb("one_col", [128, 1], F32)
    P.op("dve", lambda e: e.memset(oc[:], 1.0), writes=[b])
    consts["one_col"] = oc
    consts["invf_d"] = cd["invf"]
    P.barrier()
    return consts


def const_arrays():
    k = np.arange(128)[:, None]
    q = np.arange(128)[None, :]
    bf = ml_dtypes.bfloat16
    c = {}
    c["ident"] = (k == q).astype(bf)
    c["tri"] = np.where(q >= k, 0.0, NEG).astype(bf)
    c["tri2"] = np.where(k > q, 0.0, NEG).astype(bf)
    c["triu_f"] = (k <= q).astype(np.float32)
    c["ones_f"] = np.ones((128, 128), np.float32)
    half = 32
    c["invf"] = (10000.0 ** (-np.arange(half, dtype=np.float32) * 2.0 / 64)).astype(np.float32)
    return c


CONST_SPECS = [("ident", [128, 128], BF16), ("tri", [128, 128], BF16), ("tri2", [128, 128], BF16),
               ("triu_f", [128, 128], F32), ("ones_f", [128, 128], F32), ("invf", [32], F32)]


def attn_simple(P, kind, io, scr, consts):
    nc = P.nc
    es = contextlib.ExitStack()

    def sb(name, shape, dt):
        return es.enter_context(nc.sbuf_tensor(name, list(shape), dt))

    res = AttnRes(P, es)
    NQ = 3
    qsb = [sb("qsb%d" % i, [64, 512], BF16) for i in range(NQ)]
    qsbb = [P.buf() for _ in range(NQ)]
    szc = [sb("szc%d" % i, [128, 4, 64], BF16) for i in range(NQ)]
    szcb = [P.buf() for _ in range(NQ)]
    ogj = [sb("ogj%d" % i, [128, 4, 64], BF16) for i in range(2)]
    ogjb = [P.buf() for _ in range(2)]
    den = sb("den", [128, 8], F32)
    denb = P.buf()
    otmp = sb("otmp", [128, 4, 64], F32)
    otmpb = P.buf()
    szv = scr["sz"].rearrange("(t p) c -> p t c", p=128)
    ogv = io["og"].rearrange("(t p) c -> p t c", p=128)
    vv = scr["v"].rearrange("(t p) c -> p t c", p=128)
    jobs = []
    if kind == "swa":
        kT = sb("kT", [64, S], BF16)
        kTb = P.buf()
        P.dma("sp", kT[:], scr["tdst"][1][:, :], writes=[kTb])
        vsb = sb("vsb", [128, NT, 65], BF16)
        vsbb = P.buf()
        P.dma("sp", vsb[:], vv, writes=[vsbb])
        esk = sb("esk", [128, 8], F32)
        eskb = P.buf()
        P.dma("sp", esk[:], io["sinks"].partition_broadcast(128), writes=[eskb])
        P.op("act", lambda e: e.activation(out=esk[:], in_=esk[:], func=AF.Exp), reads=[eskb], writes=[eskb])
        order = [(c, h) for c in range(16) for h in range(8)]
    else:
        kTs = [sb("kT%d" % i, [64, S], BF16) for i in range(2)]
        kTsb = [P.buf() for _ in range(2)]
        vsb = sb("vsb", [128, NT, 8 * 65], BF16)
        vsbb = P.buf()
        P.dma("sp", vsb[:], vv, writes=[vsbb])
        cum, cumE, cumb = scr["cum"], scr["cumE"], scr["cumb"]
        bj = [sb("bj%d" % i, [128, 64], F32) for i in range(3)]
        bjb = [P.buf() for _ in range(3)]
        order = [(c, h) for h in range(8) for c in range(16)]

    for n, (c, h) in enumerate(order):
        slot = n % NQ
        job = dict(po=n % 2)

        def pre(c=c, h=h, slot=slot, n=n):
            P.dma("sp", qsb[slot][:], scr["tdst"][0][h * 64:(h + 1) * 64, c * 512:(c + 1) * 512], writes=[qsbb[slot]])
            P.dma("sp", szc[slot][:], szv[:, c * 4:(c + 1) * 4, h * 64:(h + 1) * 64], writes=[szcb[slot]])
            if kind == "fox":
                if c == 0:
                    P.dma("sp", kTs[h % 2][:], scr["tdst"][1][h * 64:(h + 1) * 64, :], writes=[kTsb[h % 2]])
                nj = 4 * c + 4
                P.op("dve", lambda e: e.tensor_scalar(out=bj[n % 3][:, 0:nj], in0=cum[:, 0:nj, h], scalar1=cumE[:, 4 * c + 3, h:h + 1], scalar2=-1.0, op0=ALU.subtract, op1=ALU.mult),
                     reads=[cumb], writes=[bjb[n % 3]])

        if kind == "swa":
            steps = window_steps(c, 128, kT, [kTb, qsbb[slot]], qsb[slot], lambda j: vsb[:, j, :], [vsbb], consts)
        else:
            steps = []
            kTh = kTs[h % 2]
            for j in range(4 * c + 4):
                jr = j - 4 * c
                if jr < 0:
                    subs, cols, masks = [0, 1, 2, 3], (0, 512), []
                else:
                    subs, cols = list(range(jr, 4)), (jr * 128, 512)
                    masks = [(consts["ident"][:], consts["tri"][:], jr * 128, (jr + 1) * 128, [])]
                steps.append(dict(kT=kTh[:, j * 128:(j + 1) * 128], q=qsb[slot], kqb=[kTsb[h % 2], qsbb[slot]], cols=cols, masks=masks, subs=subs,
                                  v=vsb[:, j, h * 65:(h + 1) * 65], vb=[vsbb], nv=65, bias=bj[n % 3][:, j:j + 1], biasb=[bjb[n % 3]]))

        def post(c=c, h=h, slot=slot, n=n):
            po = res.po[n % 2]
            pob = res.pob[n % 2]
            po3 = po[:, 0:260].rearrange("p (i v) -> p i v", v=65)
            if kind == "swa":
                P.op("dve", lambda e: e.tensor_scalar(out=den[:, 0:4], in0=po3[:, :, 64], scalar1=esk[:, h:h + 1], scalar2=0.0, op0=ALU.add, op1=ALU.add),
                     reads=[pob, eskb], writes=[denb])
                P.op("dve", lambda e: e.reciprocal(out=den[:, 4:8], in_=den[:, 0:4]), reads=[denb], writes=[denb])
            else:
                P.op("dve", lambda e: e.reciprocal(out=den[:, 4:8], in_=po3[:, :, 64]), reads=[pob], writes=[denb])
            P.op("dve", lambda e: e.tensor_tensor(out=otmp[:], in0=po3[:, :, 0:64], in1=den[:, 4:8].unsqueeze(2).to_broadcast([128, 4, 64]), op=ALU.mult),
                 reads=[pob, denb], writes=[otmpb])
            P.op("dve", lambda e: e.tensor_tensor(out=ogj[n % 2][:], in0=otmp[:], in1=szc[slot][:], op=ALU.mult),
                 reads=[otmpb, szcb[slot]], writes=[ogjb[n % 2]])
            P.dma("pool", ogv[:, c * 4:(c + 1) * 4, h * 64:(h + 1) * 64], ogj[n % 2][:], reads=[ogjb[n % 2]])

        job["pre"], job["steps"], job["post"] = pre, steps, post
        jobs.append(job)
    run_jobs(P, jobs, res)
    P.barrier()
    es.close()


def build(kind, with_prev):
    P = Prog()
    io = {}
    io["x"] = P.din("x", [S, D], F32)
    cd = {name: P.din("c_" + name, shape, dt) for (name, shape, dt) in CONST_SPECS}
    if with_prev:
        io["ogp"] = P.din("ogp", [S, D], BF16)
        io["w_out"] = P.din("w_out", [D, D], F32)
        io["xn"] = P.dout("xn", [S, D], F32)
    scr = {}
    if kind:
        cfg = KINDS[kind]
        io["pos"] = P.din("pos", [S], I32)
        io["ng"] = P.din("ng", [D], F32)
        io["w_in"] = P.din("w_in", [D, cfg["ncols"]], F32)
        io["gains"] = P.din("gains", [cfg["ngain"], 64], F32)
        io["og"] = P.dout("og", [S, 512], BF16)
        if kind == "swa":
            io["sinks"] = P.din("sinks", [8], F32)
        if kind == "fox":
            io["fbias"] = P.din("fbias", [8], F32)
            scr["cum"] = P.sb("cum", [128, NT, 8], F32)
            scr["cumE"] = P.sb("cumE", [128, NT, 8], F32)
            scr["cumb"] = P.buf()
        scr["tdst"] = [P.dscr("tdst%d" % i, [nh * 64, S], BF16) for i, (_, nh, _, _) in enumerate(cfg["titems"])]
        scr["v"] = P.dscr("vtm", [S, cfg["nvh"] * 65], BF16)
        scr["sz"] = P.dscr("sz", [S, 512], BF16)
        if kind == "nsa":
            scr["gates"] = P.dscr("gates", [S, 24], F32)
            io["kg0"] = P.din("kg0", [64, 1], F32)
            io["cmp_w1"] = P.din("cmp_w1", [2, 2048, 64], F32)
            io["cmp_w2"] = P.din("cmp_w2", [2, 64, 64], F32)
            io["cmp_posT"] = P.din("cmp_posT", [2, 64, 32], F32)
            ncd = {name: P.din("n_" + name, shape, dt) for (name, shape, dt) in NSA_CONST_SPECS}
            if DEBUG_SCRATCH:
                for k in range(3):
                    io["dbg%d" % k] = P.dout("dbg%d" % k, [S, 512], F32)
    consts = load_consts(P, cd)
    phase_p(P, kind, with_prev, io, scr, consts)
    if kind in ("swa", "fox"):
        attn_simple(P, kind, io, scr, consts)
    elif kind == "nsa":
        attn_nsa(P, io, scr, consts, ncd)
    return P.finish()


_NC_CACHE = {}


def get_nc(kind, with_prev):
    key = (kind, with_prev)
    if key not in _NC_CACHE:
        _NC_CACHE[key] = build(kind, with_prev)
    return _NC_CACHE[key]


def core_inputs(kind, j, hh, inp):
    m = {}
    if kind == "swa":
        w = inp["b_w_in"][j]
        m["w_in"] = np.ascontiguousarray(np.concatenate([w[:, hh * 512:(hh + 1) * 512], w[:, 1024 + hh * 64:1024 + (hh + 1) * 64],
                                                         w[:, 1152 + hh * 64:1152 + (hh + 1) * 64], w[:, 1280 + hh * 512:1280 + (hh + 1) * 512]], axis=1))
        m["gains"] = np.ascontiguousarray(np.stack([inp["b_q_gain"][j], inp["b_k_gain"][j]]))
        m["sinks"] = np.ascontiguousarray(inp["b_sinks"][j][hh * 8:(hh + 1) * 8])
    elif kind == "nsa":
        w = inp["a_w_in"][j]
        sl = lambda base, width: w[:, base + hh * width:base + (hh + 1) * width]
        parts = [sl(0, 512)] + [sl(1024 + 256 * i, 128) for i in (0, 2, 4, 1, 3, 5)]
        parts += [w[:, 2560 + br * 16 + hh * 8:2560 + br * 16 + (hh + 1) * 8] for br in range(3)]
        parts += [sl(2608, 512)]
        m["w_in"] = np.ascontiguousarray(np.concatenate(parts, axis=1))
        kg = inp["a_k_gain"][j]
        m["gains"] = np.ascontiguousarray(np.stack([inp["a_q_gain"][j], kg[1], kg[2]]))
        m["kg0"] = np.ascontiguousarray(kg[0].reshape(64, 1))
        m["cmp_w1"] = np.ascontiguousarray(inp["a_cmp_w1"][j])
        m["cmp_w2"] = np.ascontiguousarray(inp["a_cmp_w2"][j])
        m["cmp_posT"] = np.ascontiguousarray(np.transpose(inp["a_cmp_pos"][j], (0, 2, 1)))
        for k, v in nsa_const_arrays().items():
            m["n_" + k] = v
    elif kind == "fox":
        w = inp["c_w_in"][j]
        m["w_in"] = np.ascontiguousarray(np.concatenate([w[:, hh * 512:(hh + 1) * 512], w[:, 1024 + hh * 512:1024 + (hh + 1) * 512],
                                                         w[:, 2048 + hh * 512:2048 + (hh + 1) * 512], w[:, 3072 + hh * 8:3072 + (hh + 1) * 8],
                                                         w[:, 3088 + hh * 512:3088 + (hh + 1) * 512]], axis=1))
        m["gains"] = np.ascontiguousarray(np.stack([inp["c_q_gain"][j], inp["c_k_gain"][j]]))
        m["fbias"] = np.ascontiguousarray(inp["c_forget_bias"][j][hh * 8:(hh + 1) * 8])
    return m


def w_out_of(layer, inp):
    kind = ("nsa", "swa", "fox")[layer % 3]
    return inp[{"nsa": "a_w_out", "swa": "b_w_out", "fox": "c_w_out"}[kind]][layer // 3]


def run_layer(layer, xs, ogp, inp, cores=None):
    nb = len(xs)
    kind = ("nsa", "swa", "fox")[layer % 3] if layer < 4 else None
    with_prev = ogp is not None
    nc = get_nc(kind, with_prev)
    cst = const_arrays()
    maps = []
    for c in range(2 * nb):
        b, hh = c // 2, c % 2
        m = {"x": xs[b]}
        for k, v in cst.items():
            m["c_" + k] = v
        if with_prev:
            m["ogp"] = ogp[b]
            m["w_out"] = np.ascontiguousarray(w_out_of(layer - 1, inp))
        if kind:
            m["pos"] = np.ascontiguousarray(inp["positions"][b])
            m["ng"] = np.ascontiguousarray(inp["norm_gains"][layer])
            m.update(core_inputs(kind, layer // 3, hh, inp))
        maps.append(m)
    res = run_bass_kernel_spmd(nc, maps, core_ids=list(range(2 * nb)))
    r = res.results
    new_xs = [r[2 * b]["xn"] for b in range(nb)] if with_prev else xs
    og = [np.concatenate([r[2 * b]["og"], r[2 * b + 1]["og"]], axis=1) for b in range(nb)] if kind else None
    return new_xs, og


NSA_CONST_SPECS = [("Eall", [128, S], BF16), ("Tm", [128, 3088], BF16), ("Gk", [128, 254], F32), ("Ga", [128, 254], F32),
                   ("ovl", [128, 4, 129], BF16), ("ones_b", [64, 64], BF16)]


def nsa_const_arrays():
    bf = ml_dtypes.bfloat16
    c = {}
    n = np.arange(128)[:, None]
    k = np.arange(S)[None, :]
    c["Eall"] = (k // 64 == n).astype(bf)
    p = np.arange(128)[:, None]
    u = np.arange(3088)[None, :] - 512
    c["Tm"] = np.where(u >= 16 * p + 31, 0.0, NEG).astype(bf)
    m = np.arange(254)[None, :]
    d = m - 126 - (p >= 64)
    c["Gk"] = (d <= -2).astype(np.float32)
    ga = np.zeros((128, 254), np.float32)
    ga[d == -1] = 1e30
    ga[d == 0] = 2e30
    ga[d > 0] = -1e30
    c["Ga"] = ga
    ovl = np.zeros((128, 4, 129), np.float32)
    for cc in range(511):
        a, pp = cc // 128, cc % 128
        ovl[pp, a, 0] = 1.0
        nb, j = cc // 4, cc % 4
        if j <= 2:
            ovl[pp, a, 1 + nb] = 1.0
        else:
            ovl[pp, a, 1 + nb] = 0.5
            if nb + 1 < 128:
                ovl[pp, a, 1 + nb + 1] = 0.5
    c["ovl"] = ovl.astype(bf)
    c["ones_b"] = np.ones((64, 64), bf)
    return c


def attn_nsa(P, io, scr, consts, cd):
    nc = P.nc
    es = contextlib.ExitStack()

    def sb(name, shape, dt):
        return es.enter_context(nc.sbuf_tensor(name, list(shape), dt))

    ident = consts["ident"]
    cb = P.buf()
    Eall = sb("Eall", [128, S], BF16)
    Tm = sb("Tm", [128, 3088], BF16)
    Gk = sb("Gk", [128, 254], F32)
    Ga = sb("Ga", [128, 254], F32)
    ovl = sb("ovl", [128, 4, 129], BF16)
    ones_b = sb("ones_b", [64, 64], BF16)
    for t, nm in ((Eall, "Eall"), (Tm, "Tm"), (Gk, "Gk"), (Ga, "Ga"), (ovl, "ovl"), (ones_b, "ones_b")):
        P.dma("sp", t[:], cd[nm], writes=[cb])
    kcmpT = [sb("kcmpT%d" % g, [64, 512], BF16) for g in range(2)]
    vca = [sb("vca%d" % g, [128, 4, 193], BF16) for g in range(2)]
    cmpb = P.buf()
    vsb = sb("vsb", [128, NT, 260], BF16)
    vsbb = P.buf()
    P.dma("sp", vsb[:], scr["v"].rearrange("(t p) c -> p t c", p=128), writes=[vsbb])

    with contextlib.ExitStack() as es2:
        def sb2(name, shape, dt):
            return es2.enter_context(nc.sbuf_tensor(name, list(shape), dt))
        pc = [es2.enter_context(nc.psum_tensor("pc%d" % i, [128, 512], F32)) for i in range(4)]
        pcb = [P.buf() for _ in range(4)]
        tT = sb2("tT", [64, S], BF16)
        tTb = P.buf()
        w1f = sb2("w1f", [64, 32, 64], F32)
        w1s = sb2("w1s", [64, 32, 64], BF16)
        w1b = P.buf()
        w2f = sb2("w2f", [64, 64], F32)
        w2s = sb2("w2s", [64, 64], BF16)
        w2b = P.buf()
        pf = sb2("pf", [64, 32], F32)
        pb = sb2("pb", [64, 32], BF16)
        pbb = P.buf()
        cvec = sb2("cvec", [64, 1], F32)
        cvb = P.buf()
        kg0 = sb2("kg0s", [64, 1], F32)
        kg0b = P.buf()
        P.dma("sp", kg0[:], io["kg0"], writes=[kg0b])
        xg = sb2("xg", [64, 512], F32)
        x2 = sb2("x2", [64, 512], F32)
        gsb = sb2("gsb", [64, 512], BF16)
        xb = P.buf()
        for g in range(2):
            P.op("pool", lambda e, g=g: e.memset(kcmpT[g][:], 0.0), writes=[cmpb])
            P.op("pool", lambda e, g=g: e.memset(vca[g][:], 0.0), writes=[cmpb])
            P.op("dve", lambda e, g=g: e.tensor_copy(out=vca[g][:, :, 64:193], in_=ovl[:]), reads=[cb], writes=[cmpb])
        for g in range(2):
            for kv in range(2):
                src = scr["tdst"][1] if kv == 0 else scr["tdst"][4]
                P.dma("sp", tT[:], src[g * 64:(g + 1) * 64, :], writes=[tTb])
                P.dma("sp", w1f[:], io["cmp_w1"][kv].rearrange("(l d) e -> d l e", d=64), writes=[w1b])
                P.op("dve", lambda e: e.tensor_copy(out=w1s[:], in_=w1f[:]), reads=[w1b], writes=[w1b])
                P.dma("sp", w2f[:], io["cmp_w2"][kv], writes=[w2b])
                P.op("dve", lambda e: e.tensor_copy(out=w2s[:], in_=w2f[:]), reads=[w2b], writes=[w2b])
                P.dma("sp", pf[:], io["cmp_posT"][kv], writes=[pbb])
                P.op("dve", lambda e: e.tensor_copy(out=pb[:], in_=pf[:]), reads=[pbb], writes=[pbb])
                for l in range(32):
                    P.op("pe", lambda e, l=l: e.matmul(pc[0][0:64, 0:1], lhsT=w1s[:, l, :], rhs=pb[:, l:l + 1], start=(l == 0), stop=(l == 31)),
                         reads=[w1b, pbb], writes=[pcb[0]])
                P.op("act", lambda e: e.copy(out=cvec[:], in_=pc[0][0:64, 0:1]), reads=[pcb[0]], writes=[cvb])
                for l in range(32):
                    P.op("pe", lambda e, l=l: e.matmul(pc[1][0:64, 0:511], lhsT=w1s[:, l, :], rhs=tT[:, l:l + 16 * 510 + 1:16], start=(l == 0), stop=(l == 31)),
                         reads=[w1b, tTb], writes=[pcb[1]])
                P.op("act", lambda e: e.activation(out=xg[:, 0:511], in_=pc[1][0:64, 0:511], func=AF.Identity, bias=cvec[:, 0:1], scale=1.0),
                     reads=[pcb[1], cvb], writes=[xb])
                P.op("dve", lambda e: e.tensor_tensor(out=x2[:, 0:511], in0=xg[:, 0:511], in1=xg[:, 0:511], op=ALU.mult), reads=[xb], writes=[xb])
                P.op("dve", lambda e: e.tensor_scalar(out=x2[:, 0:511], in0=x2[:, 0:511], scalar1=0.044715, scalar2=1.0, op0=ALU.mult, op1=ALU.add), reads=[xb], writes=[xb])
                P.op("dve", lambda e: e.tensor_tensor(out=x2[:, 0:511], in0=x2[:, 0:511], in1=xg[:, 0:511], op=ALU.mult), reads=[xb], writes=[xb])
                P.op("act", lambda e: e.activation(out=x2[:, 0:511], in_=x2[:, 0:511], func=AF.Tanh, scale=0.7978845608028654), reads=[xb], writes=[xb])
                P.op("dve", lambda e: e.tensor_scalar(out=x2[:, 0:511], in0=x2[:, 0:511], scalar1=0.5, scalar2=0.5, op0=ALU.mult, op1=ALU.add), reads=[xb], writes=[xb])
                P.op("dve", lambda e: e.tensor_tensor(out=gsb[:, 0:511], in0=x2[:, 0:511], in1=xg[:, 0:511], op=ALU.mult), reads=[xb], writes=[xb])
                P.op("pe", lambda e: e.matmul(pc[2][0:64, 0:511], lhsT=w2s[:], rhs=gsb[:, 0:511], start=True, stop=True), reads=[w2b, xb], writes=[pcb[2]])
                if kv == 0:
                    P.op("act", lambda e: e.activation(out=gsb[:, 0:511], in_=pc[2][0:64, 0:511], func=AF.Square), reads=[pcb[2], xb], writes=[xb])
                    P.op("pe", lambda e: e.matmul(pc[3][0:64, 0:511], lhsT=ones_b[:], rhs=gsb[:, 0:511], start=True, stop=True), reads=[cb, xb], writes=[pcb[3]])
                    P.op("dve", lambda e: e.tensor_scalar(out=x2[:, 0:511], in0=pc[3][0:64, 0:511], scalar1=1.0 / 64, scalar2=EPS, op0=ALU.mult, op1=ALU.add),
                         reads=[pcb[3], xb], writes=[xb])
                    P.op("act", lambda e: e.activation(out=x2[:, 0:511], in_=x2[:, 0:511], func=AF.Ln), reads=[xb], writes=[xb])
                    P.op("act", lambda e: e.activation(out=x2[:, 0:511], in_=x2[:, 0:511], func=AF.Exp, scale=-0.5), reads=[xb], writes=[xb])
                    P.op("dve", lambda e: e.tensor_tensor(out=xg[:, 0:511], in0=pc[2][0:64, 0:511], in1=x2[:, 0:511], op=ALU.mult), reads=[pcb[2], xb], writes=[xb])
                    P.op("dve", lambda e, g=g: e.tensor_scalar(out=kcmpT[g][:, 0:511], in0=xg[:, 0:511], scalar1=kg0[:, 0:1], scalar2=0.0, op0=ALU.mult, op1=ALU.add),
                         reads=[xb, kg0b], writes=[cmpb])
                else:
                    P.op("pool", lambda e: e.memset(gsb[:], 0.0), reads=[xb], writes=[xb])
                    P.op("act", lambda e: e.copy(out=gsb[:, 0:511], in_=pc[2][0:64, 0:511]), reads=[pcb[2], xb], writes=[xb])
                    pv = pc[3][:].bitcast(BF16)
                    for a in range(4):
                        P.op("pe", lambda e, a=a: e.transpose(out=pv[:, a * 64:(a + 1) * 64], in_=gsb[:, a * 128:(a + 1) * 128], identity=ident[0:64, 0:64]),
                             reads=[xb], writes=[pcb[3]])
                    P.op("act", lambda e, g=g: e.copy(out=vca[g][:, :, 0:64], in_=pv[:, 0:256].rearrange("p (a d) -> p a d", d=64)), reads=[pcb[3]], writes=[cmpb])
        P.barrier()

    res = AttnRes(P, es)
    pX = es.enter_context(nc.psum_tensor("pX", [128, 512], F32))
    pXb = P.buf()
    NQ = 3
    q4 = [sb("q4_%d" % i, [64, 4, 512], BF16) for i in range(NQ)]
    q4b = [P.buf() for _ in range(NQ)]
    gt = [sb("gt%d" % i, [128, 4, 24], F32) for i in range(NQ)]
    gtb = [P.buf() for _ in range(NQ)]
    szc = [sb("szc%d" % i, [128, 4, 256], BF16) for i in range(NQ)]
    szcb = [P.buf() for _ in range(NQ)]
    ksT2 = [sb("ksT%d" % i, [64, S], BF16) for i in range(2)]
    kwT2 = [sb("kwT%d" % i, [64, S], BF16) for i in range(2)]
    kb2 = [P.buf() for _ in range(2)]
    acc = [sb("acc%d" % i, [128, 4, 4, 64], F32) for i in range(2)]
    accb = [[P.buf() for _ in range(4)] for _ in range(2)]
    imp = [sb("imp%d" % i, [128, 4, 128], F32) for i in range(2)]
    impb = [P.buf() for _ in range(2)]
    MnT = [sb("MnT%d" % i, [128, 512], BF16) for i in range(2)]
    MnTb = [P.buf() for _ in range(2)]
    st_ = sb("st_", [128, 32], F32)
    stb = P.buf()
    otmp = sb("otmp", [128, 4, 64], F32)
    otmpb = P.buf()
    itmp = sb("itmp", [128, 2, 128], F32)
    itmpb = P.buf()
    stmp = [sb("stmp%d" % i, [128, 128], F32) for i in range(4)]
    swk = [sb("swk%d" % i, [128, 128], F32) for i in range(4)]
    sm8 = [sb("sm8%d" % i, [128, 16], F32) for i in range(4)]
    smn = [sb("smn%d" % i, [128, 128], BF16) for i in range(4)]
    selb = [P.buf() for _ in range(4)]
    ogj = [sb("ogj%d" % i, [128, 4, 64], BF16) for i in range(2)]
    ogjb = [P.buf() for _ in range(2)]
    tiny = sb("tiny", [128, 1], F32)
    P.op("dve", lambda e: e.memset(tiny[:], 1e-30), writes=[stb])
    szv = scr["sz"].rearrange("(t p) c -> p t c", p=128)
    ogv = io["og"].rearrange("(t p) c -> p t c", p=128)
    gv = scr["gates"].rearrange("(t p) c -> p t c", p=128)
    dbgv = [io["dbg%d" % k].rearrange("(t p) c -> p t c", p=128) for k in range(3)] if DEBUG_SCRATCH else None

    jobs = []
    cnt = [0]
    for g in range(2):
        ksT, kwT, kb = ksT2[g], kwT2[g], kb2[g]
        for c in range(16):
            gc = g * 16 + c
            slot = gc % NQ
            par = gc % 2
            amax = (32 * c + 30) // 128

            def pre_chunk(g=g, c=c, slot=slot, ksT=ksT, kwT=kwT, kb=kb):
                if c == 0:
                    P.dma("sp", ksT[:], scr["tdst"][2][g * 64:(g + 1) * 64, :], writes=[kb])
                    P.dma("sp", kwT[:], scr["tdst"][3][g * 64:(g + 1) * 64, :], writes=[kb])
                for rr in range(4):
                    P.dma("sp", q4[slot][:, rr, :], scr["tdst"][0][g * 256 + rr * 64:g * 256 + (rr + 1) * 64, c * 512:(c + 1) * 512], writes=[q4b[slot]])
                P.dma("sp", gt[slot][:], gv[:, c * 4:(c + 1) * 4, :], writes=[gtb[slot]])
                P.dma("sp", szc[slot][:], szv[:, c * 4:(c + 1) * 4, g * 256:(g + 1) * 256], writes=[szcb[slot]])
                if EXPERIMENT_A:
                    for e in ("pe", "act", "dve", "pool"):
                        for b in (q4b[slot], gtb[slot], szcb[slot]):
                            P._wait(e, b.w)

            def scale_of(po3, ncol, nsub, i0, gcol, slot, pob):
                P.op("dve", lambda e: e.tensor_scalar(out=st_[:, 0:nsub], in0=po3[:, :, ncol], scalar1=tiny[:, 0:1], scalar2=0.0, op0=ALU.add, op1=ALU.add),
                     reads=[pob, stb], writes=[stb])
                P.op("dve", lambda e: e.reciprocal(out=st_[:, 4:4 + nsub], in_=st_[:, 0:nsub]), reads=[stb], writes=[stb])
                P.op("dve", lambda e: e.tensor_tensor(out=st_[:, 8:8 + nsub], in0=st_[:, 4:4 + nsub], in1=gt[slot][:, i0:i0 + nsub, gcol], op=ALU.mult),
                     reads=[stb, gtb[slot]], writes=[stb])

            for r in range(4):
                hl = g * 4 + r
                for half in range(2):
                    i0 = half * 2
                    n = cnt[0]
                    cnt[0] += 1
                    steps = []
                    for a in range(amax + 1):
                        u0 = 512 * c - 2048 * a
                        masks = []
                        if u0 < 2063:
                            off = 512 + u0 + i0 * 128
                            masks = [(ident[:], Tm[:, off:off + 256], i0 * 128, i0 * 128 + 256, [cb])]
                        steps.append(dict(kT=kcmpT[g][:, a * 128:(a + 1) * 128], q=q4[slot][:, r, :], kqb=[cmpb, q4b[slot]], cols=(i0 * 128, i0 * 128 + 256),
                                          masks=masks, subs=[i0, i0 + 1], v=vca[g][:, a, :], vb=[cmpb], nv=193))

                    def post(r=r, hl=hl, i0=i0, n=n, slot=slot, par=par, g=g, c=c):
                        po = res.po[n % 2]
                        pob = res.pob[n % 2]
                        po3 = po[:, 0:386].rearrange("p (i v) -> p i v", v=193)
                        scale_of(po3, 64, 2, i0, hl, slot, pob)
                        P.op("dve", lambda e: e.tensor_tensor(out=acc[par][:, r, i0:i0 + 2, :], in0=po3[:, :, 0:64], in1=st_[:, 8:10].unsqueeze(2).to_broadcast([128, 2, 64]), op=ALU.mult),
                             reads=[pob, stb], writes=[accb[par][r]])
                        if DEBUG_SCRATCH:
                            P.dma("pool", dbgv[0][:, c * 4 + i0:c * 4 + i0 + 2, hl * 64:(hl + 1) * 64], acc[par][:, r, i0:i0 + 2, :], reads=[accb[par][r]])
                        if r == 0:
                            P.op("dve", lambda e: e.tensor_tensor(out=imp[par][:, i0:i0 + 2, :], in0=po3[:, :, 65:193], in1=st_[:, 4:6].unsqueeze(2).to_broadcast([128, 2, 128]), op=ALU.mult),
                                 reads=[pob, stb], writes=[impb[par]])
                        else:
                            P.op("dve", lambda e: e.tensor_tensor(out=itmp[:], in0=po3[:, :, 65:193], in1=st_[:, 4:6].unsqueeze(2).to_broadcast([128, 2, 128]), op=ALU.mult),
                                 reads=[pob, stb], writes=[itmpb])
                            P.op("pool", lambda e: e.tensor_tensor(out=imp[par][:, i0:i0 + 2, :], in0=imp[par][:, i0:i0 + 2, :], in1=itmp[:], op=ALU.add),
                                 reads=[itmpb, impb[par]], writes=[impb[par]])
                        if r == 3 and i0 == 2:
                            def each(fn, rd, wr):
                                for i in range(4):
                                    P.op("dve", lambda e, i=i: fn(e, i), reads=rd(i), writes=wr(i))
                            sl = lambda i: slice(126 - 2 * (4 * c + i), 254 - 2 * (4 * c + i))
                            each(lambda e, i: e.tensor_tensor(out=stmp[i][:], in0=imp[par][:, i, :], in1=Gk[:, sl(i)], op=ALU.mult), lambda i: [impb[par], cb], lambda i: [selb[i]])
                            each(lambda e, i: e.tensor_tensor(out=stmp[i][:], in0=stmp[i][:], in1=Ga[:, sl(i)], op=ALU.add), lambda i: [selb[i], cb], lambda i: [selb[i]])
                            each(lambda e, i: e.memset(stmp[i][:, 0:1], 3e30), lambda i: [selb[i]], lambda i: [selb[i]])
                            each(lambda e, i: e.max(out=sm8[i][:, 0:8], in_=stmp[i][:]), lambda i: [selb[i]], lambda i: [selb[i]])
                            each(lambda e, i: e.match_replace(out=swk[i][:], in_to_replace=sm8[i][:, 0:8], in_values=stmp[i][:], imm_value=-3e38), lambda i: [selb[i]], lambda i: [selb[i]])
                            each(lambda e, i: e.max(out=sm8[i][:, 8:16], in_=swk[i][:]), lambda i: [selb[i]], lambda i: [selb[i]])
                            each(lambda e, i: e.tensor_scalar(out=smn[i][:], in0=stmp[i][:], scalar1=sm8[i][:, 15:16], scalar2=NEG, op0=ALU.is_lt, op1=ALU.mult),
                                 lambda i: [selb[i]], lambda i: [selb[i]])

                    jobs.append(dict(pre=(pre_chunk if (r == 0 and half == 0) else (lambda: None)), steps=steps, post=post, po=n % 2, sub0=i0))

            for r in range(4):
                hl = g * 4 + r
                n = cnt[0]
                cnt[0] += 1
                steps = window_steps(c, 512, kwT, [kb, q4b[slot]], q4[slot][:, r, :], lambda j, g=g: vsb[:, j, (2 + g) * 65:(3 + g) * 65], [vsbb], consts)

                def post(r=r, hl=hl, n=n, slot=slot, par=par, c=c):
                    po = res.po[n % 2]
                    pob = res.pob[n % 2]
                    po3 = po[:, 0:260].rearrange("p (i v) -> p i v", v=65)
                    scale_of(po3, 64, 4, 0, 16 + hl, slot, pob)
                    P.op("dve", lambda e: e.tensor_tensor(out=otmp[:], in0=po3[:, :, 0:64], in1=st_[:, 8:12].unsqueeze(2).to_broadcast([128, 4, 64]), op=ALU.mult),
                         reads=[pob, stb], writes=[otmpb])
                    if DEBUG_SCRATCH:
                        P.dma("pool", dbgv[2][:, c * 4:c * 4 + 4, hl * 64:(hl + 1) * 64], otmp[:], reads=[otmpb])
                    P.op("pool", lambda e: e.tensor_tensor(out=acc[par][:, r, :, :], in0=acc[par][:, r, :, :], in1=otmp[:], op=ALU.add),
                         reads=[otmpb, accb[par][r]], writes=[accb[par][r]])
                    if r == 2:

                        pXv = pX[:].bitcast(BF16)
                        for i in range(4):
                            P.op("pe", lambda e, i=i: e.transpose(out=pXv[:, i * 128:(i + 1) * 128], in_=smn[i][:], identity=ident[:]), reads=[selb[i]], writes=[pXb])
                        P.op("act", lambda e: e.copy(out=MnT[par][:], in_=pXv[:, 0:512]), reads=[pXb], writes=[MnTb[par]])

                jobs.append(dict(pre=(lambda: None), steps=steps, post=post, po=n % 2))

            for r in range(4):
                hl = g * 4 + r
                n = cnt[0]
                cnt[0] += 1
                steps = []
                for j in range(4 * c + 4):
                    jr = j - 4 * c
                    if jr < 0:
                        subs, cols = [0, 1, 2, 3], (0, 512)
                        masks = [(Eall[:, j * 128:(j + 1) * 128], MnT[par][:, 0:512], 0, 512, [cb, MnTb[par]])]
                    else:
                        subs, cols = list(range(jr, 4)), (jr * 128, 512)
                        masks = [(Eall[:, j * 128:(j + 1) * 128], MnT[par][:, jr * 128:512], jr * 128, 512, [cb, MnTb[par]]),
                                 (ident[:], consts["tri"][:], jr * 128, (jr + 1) * 128, [])]
                    steps.append(dict(kT=ksT[:, j * 128:(j + 1) * 128], q=q4[slot][:, r, :], kqb=[kb, q4b[slot]], cols=cols, masks=masks, subs=subs,
                                      v=vsb[:, j, g * 65:(g + 1) * 65], vb=[vsbb], nv=65))

                def post(r=r, hl=hl, n=n, slot=slot, par=par, c=c):
                    po = res.po[n % 2]
                    pob = res.pob[n % 2]
                    po3 = po[:, 0:260].rearrange("p (i v) -> p i v", v=65)
                    scale_of(po3, 64, 4, 0, 8 + hl, slot, pob)
                    P.op("dve", lambda e: e.tensor_tensor(out=otmp[:], in0=po3[:, :, 0:64], in1=st_[:, 8:12].unsqueeze(2).to_broadcast([128, 4, 64]), op=ALU.mult),
                         reads=[pob, stb], writes=[otmpb])
                    if DEBUG_SCRATCH:
                        P.dma("pool", dbgv[1][:, c * 4:c * 4 + 4, hl * 64:(hl + 1) * 64], otmp[:], reads=[otmpb])
                    P.op("pool", lambda e: e.tensor_tensor(out=acc[par][:, r, :, :], in0=acc[par][:, r, :, :], in1=otmp[:], op=ALU.add),
                         reads=[otmpb, accb[par][r]], writes=[accb[par][r]])
                    P.op("pool", lambda e: e.tensor_tensor(out=ogj[n % 2][:], in0=acc[par][:, r, :, :], in1=szc[slot][:, :, r * 64:(r + 1) * 64], op=ALU.mult),
                         reads=[accb[par][r], szcb[slot]], writes=[ogjb[n % 2]])
                    P.dma("pool", ogv[:, c * 4:(c + 1) * 4, hl * 64:(hl + 1) * 64], ogj[n % 2][:], reads=[ogjb[n % 2]])

                jobs.append(dict(pre=(lambda: None), steps=steps, post=post, po=n % 2))
    run_jobs(P, jobs, res)
    P.barrier()
    es.close()


def kernel(**inputs):
    inp = {k: np.asarray(v) for k, v in inputs.items()}
    nb = inp["x"].shape[0]
    xs = [np.ascontiguousarray(inp["x"][b], dtype=np.float32) for b in range(nb)]
    og = None
    for layer in range(4):
        xs, og = run_layer(layer, xs, og, inp)
    xs, _ = run_layer(4, xs, og, inp)
    return np.stack(xs).astype(np.float32)
```
